# Optimizing a Trainium2 kernel written in Bass

```python
import math
import jax, jax.numpy as jnp
from jax import lax
import numpy as np

D_MODEL = 1024
BATCH = 16
SEQ = 2048
DEPTH = 1

ATTN_HEADS = 8
HEAD_DIM = 64
ATTN_WIDTH = ATTN_HEADS * HEAD_DIM
GMLP_GROUPS = 4
GMLP_GROUP_DIM = 128
GMLP_WIDTH = GMLP_GROUPS * GMLP_GROUP_DIM
MIX_WIDTH = ATTN_WIDTH + GMLP_WIDTH
IN_WIDTH = 3 * ATTN_WIDTH + 2 * GMLP_WIDTH
CHUNK = 128
WINDOW_DILATIONS = ((128, 1), (512, 4), (2048, 16))
BLOCK = 128
D_FF = 4 * D_MODEL
EPS = 1e-6

kernel_name = "hybrid_dilated_attn_gmlp_block"


def rms_norm(x, g):
    xf = x.astype(jnp.float32)
    y = xf * lax.rsqrt(jnp.mean(xf * xf, axis=-1, keepdims=True) + EPS)
    return (y * g.astype(jnp.float32)).astype(x.dtype)


def layer_norm(x, g, b):
    xf = x.astype(jnp.float32)
    mu = jnp.mean(xf, axis=-1, keepdims=True)
    var = jnp.mean(jnp.square(xf - mu), axis=-1, keepdims=True)
    y = (xf - mu) * lax.rsqrt(var + EPS)
    return (y * g.astype(jnp.float32) + b.astype(jnp.float32)).astype(x.dtype)


def causal_window_attention(q, k, v, span):
    N, L, H, D = q.shape
    Lp = -(-L // BLOCK) * BLOCK
    pad = Lp - L
    if pad:
        pw = ((0, 0), (0, pad), (0, 0), (0, 0))
        q, k, v = jnp.pad(q, pw), jnp.pad(k, pw), jnp.pad(v, pw)
    nb = Lp // BLOCK
    qb = q.reshape(N, nb, BLOCK, H, D)
    kb = k.reshape(N, nb, BLOCK, H, D)
    vb = v.reshape(N, nb, BLOCK, H, D)
    k_prev = jnp.concatenate([jnp.zeros_like(kb[:, :1]), kb[:, :-1]], axis=1)
    v_prev = jnp.concatenate([jnp.zeros_like(vb[:, :1]), vb[:, :-1]], axis=1)
    kk = jnp.concatenate([k_prev, kb], axis=2)
    vv = jnp.concatenate([v_prev, vb], axis=2)
    scale = 1.0 / math.sqrt(D)
    s = jnp.einsum('nbqhd,nbkhd->nbhqk', qb, kk).astype(jnp.float32) * scale
    qpos = jnp.arange(BLOCK)[:, None] + BLOCK
    kpos = jnp.arange(2 * BLOCK)[None, :]
    dist = qpos - kpos
    band = (dist >= 0) & (dist <= span)
    first = (jnp.arange(nb)[:, None, None] == 0) & (kpos[None] < BLOCK)
    mask = band[None] & ~first
    s = jnp.where(mask[None, :, None], s, -jnp.inf)
    m = jnp.max(s, axis=-1, keepdims=True)
    p = jnp.exp(s - m)
    den = jnp.sum(p, axis=-1)
    o = jnp.einsum('nbhqk,nbkhd->nbqhd', p.astype(vv.dtype), vv).astype(jnp.float32)
    o = o / jnp.transpose(den, (0, 1, 3, 2))[..., None]
    lse = m[..., 0] + jnp.log(den)
    o = o.reshape(N, Lp, H, D)[:, :L]
    lse = jnp.transpose(lse, (0, 1, 3, 2)).reshape(N, Lp, H)[:, :L]
    return o, lse


def dilated_branch(q, k, v, window, dilation):
    B, S, H, D = q.shape
    L = S // dilation

    def to_residue(t):
        return t.reshape(B, L, dilation, H, D).transpose(0, 2, 1, 3, 4).reshape(B * dilation, L, H, D)

    o, lse = causal_window_attention(to_residue(q), to_residue(k), to_residue(v), window // dilation)
    o = o.reshape(B, dilation, L, H, D).transpose(0, 2, 1, 3, 4).reshape(B, S, H, D)
    lse = lse.reshape(B, dilation, L, H).transpose(0, 2, 1, 3).reshape(B, S, H)
    return o, lse


def dilated_attention(q, k, v):
    outs, lses = [], []
    for window, dilation in WINDOW_DILATIONS:
        o, lse = dilated_branch(q, k, v, window, dilation)
        outs.append(o)
        lses.append(lse)
    o = jnp.stack(outs, axis=0)
    w = jax.nn.softmax(jnp.stack(lses, axis=0), axis=0)
    return jnp.sum(w[..., None] * o, axis=0)


def spatial_gating(u, g, ln_g, ln_b, w_s, b_s):
    B, S, _ = u.shape
    nc = S // CHUNK
    g = g.reshape(B, S, GMLP_GROUPS, GMLP_GROUP_DIM)
    g = layer_norm(g, ln_g, ln_b)
    g = g.reshape(B, nc, CHUNK, GMLP_GROUPS, GMLP_GROUP_DIM)
    causal = jnp.tril(jnp.ones((CHUNK, CHUNK), dtype=bool))
    w_m = jnp.where(causal[None], w_s, 0.0).astype(g.dtype)
    z = jnp.einsum('gts,bnsgc->bntgc', w_m, g) + b_s.T[None, None, :, :, None]
    z = z.reshape(B, S, GMLP_WIDTH)
    return u * z


def setup_inputs(seed: int = 0) -> dict:
    key = jax.random.key(seed)
    ks = jax.random.split(key, 16)
    f32 = jnp.float32

    def nrm(k, shape, scale):
        return jax.random.normal(k, shape, f32) * scale

    L = DEPTH
    return {
        "x": jax.random.normal(ks[0], (BATCH, SEQ, D_MODEL), f32),
        "norm1_g": 1.0 + nrm(ks[1], (L, D_MODEL), 0.02),
        "w_in": nrm(ks[2], (L, D_MODEL, IN_WIDTH), D_MODEL ** -0.5),
        "q_norm_g": 1.0 + nrm(ks[3], (L, HEAD_DIM), 0.02),
        "k_norm_g": 1.0 + nrm(ks[4], (L, HEAD_DIM), 0.02),
        "ln_v_g": 1.0 + nrm(ks[5], (L, GMLP_GROUPS, GMLP_GROUP_DIM), 0.02),
        "ln_v_b": nrm(ks[6], (L, GMLP_GROUPS, GMLP_GROUP_DIM), 0.02),
        "w_spatial": nrm(ks[7], (L, GMLP_GROUPS, CHUNK, CHUNK), CHUNK ** -0.5),
        "b_spatial": 1.0 + nrm(ks[8], (L, GMLP_GROUPS, CHUNK), 0.1),
        "attn_out_g": 1.0 + nrm(ks[9], (L, ATTN_WIDTH), 0.02),
        "gmlp_out_g": 1.0 + nrm(ks[10], (L, GMLP_WIDTH), 0.02),
        "w_out": nrm(ks[11], (L, MIX_WIDTH, D_MODEL), MIX_WIDTH ** -0.5),
        "norm2_g": 1.0 + nrm(ks[12], (L, D_MODEL), 0.02),
        "w_ff1": nrm(ks[13], (L, D_MODEL, D_FF), D_MODEL ** -0.5),
        "w_ff2": nrm(ks[14], (L, D_FF, D_MODEL), D_FF ** -0.5),
    }


def reference(x, norm1_g, w_in, q_norm_g, k_norm_g, ln_v_g, ln_v_b, w_spatial,
              b_spatial, attn_out_g, gmlp_out_g, w_out, norm2_g, w_ff1, w_ff2):
    B, S, _ = x.shape
    for l in range(DEPTH):
        h = rms_norm(x, norm1_g[l])
        proj = jnp.einsum('bsd,de->bse', h, w_in[l])
        q, k, v, u, g = jnp.split(
            proj, np.cumsum([ATTN_WIDTH] * 3 + [GMLP_WIDTH]).tolist(), axis=-1)
        q = rms_norm(q.reshape(B, S, ATTN_HEADS, HEAD_DIM), q_norm_g[l])
        k = rms_norm(k.reshape(B, S, ATTN_HEADS, HEAD_DIM), k_norm_g[l])
        v = v.reshape(B, S, ATTN_HEADS, HEAD_DIM)
        a = dilated_attention(q, k, v).astype(x.dtype).reshape(B, S, ATTN_WIDTH)
        m = spatial_gating(jax.nn.gelu(u), jax.nn.gelu(g), ln_v_g[l], ln_v_b[l],
                           w_spatial[l], b_spatial[l])
        mix = jnp.concatenate([rms_norm(a, attn_out_g[l]), rms_norm(m, gmlp_out_g[l])], axis=-1)
        x = x + jnp.einsum('bse,ed->bsd', mix, w_out[l])
        h = rms_norm(x, norm2_g[l])
        f = jnp.square(jax.nn.relu(jnp.einsum('bsd,df->bsf', h, w_ff1[l])))
        x = x + jnp.einsum('bsf,fd->bsd', f, w_ff2[l])
    return x
```

```python
import numpy as np
import ml_dtypes
import concourse.bass as bass
import concourse.mybir as mybir
from concourse.bass_utils import run_bass_kernel_spmd

F32 = mybir.dt.float32
BF16 = mybir.dt.bfloat16
AF = mybir.ActivationFunctionType
ALU = mybir.AluOpType
AX = mybir.AxisListType

NCORES = 8
TOK = 4096
D = 1024
NT = TOK // 128
DFF = 4096
INW = 2560
EPS = 1e-6
GC = 0.7978845608028654
GA = 0.044715


class _Op:
    __slots__ = ("eng", "emit", "deps", "inc", "ticket", "dsem", "n_dma")


class Sched:
    ENGS = ("pe", "act", "dve", "pool", "sp")

    def __init__(self):
        self.ops = []
        self.lastw = {}
        self.readers = {}
        self.last_on = {}

    @staticmethod
    def _key(op):
        return ("d", op.dsem) if op.dsem is not None else ("e", op.eng)

    def add(self, eng, emit, reads=(), writes=(), dsem=None, n_dma=1, extra_deps=()):
        op = _Op()
        op.eng, op.emit, op.dsem, op.n_dma = eng, emit, dsem, n_dma
        op.inc = dsem is not None
        op.ticket = None
        deps = set(extra_deps)
        writes = list(writes) + [r for r in reads if r.startswith("ps")]
        reads = [r for r in reads if not r.startswith("ps")]
        for r in reads:
            w = self.lastw.get(r)
            if w is not None:
                deps.add(w)
        for w_ in writes:
            w = self.lastw.get(w_)
            if w is not None:
                deps.add(w)
            for rd in self.readers.get(w_, {}).values():
                deps.add(rd)
        op.deps = [d for d in deps
                   if not (d.eng == "pe" and eng == "pe" and d.dsem is None and dsem is None)]
        for d in op.deps:
            d.inc = True
        for r in reads:
            self.readers.setdefault(r, {})[self._key(op)] = op
        for w_ in writes:
            self.lastw[w_] = op
            self.readers[w_] = {}
        self.ops.append(op)
        self.last_on[self._key(op)] = op
        return op

    def barrier(self):
        lasts = list(self.last_on.values())
        for e in self.ENGS:
            self.add(e, None, extra_deps=[l for l in lasts])

    def finalize(self):
        cnt = {}
        for op in self.ops:
            k = self._key(op)
            if op.dsem is not None:
                cnt[k] = cnt.get(k, 0) + op.n_dma
                op.ticket = cnt[k]
            elif op.inc and op.emit is not None:
                cnt[k] = cnt.get(k, 0) + 1
                op.ticket = cnt[k]
            else:
                op.ticket = cnt.get(k, 0)
        return cnt

    def emit_engine(self, eng_name, eng, esems, dsems):
        waited = {}
        for op in self.ops:
            if op.eng != eng_name:
                continue
            need = {}
            for d in op.deps:
                k = self._key(d)
                v = d.ticket * 16 if d.dsem is not None else d.ticket
                if v > need.get(k, 0):
                    need[k] = v
            for k, v in need.items():
                if v <= 0:
                    continue
                if waited.get(k, 0) < v:
                    sem = dsems[k[1]] if k[0] == "d" else esems[k[1]]
                    eng.wait_ge(sem, v)
                    waited[k] = v
            if op.emit is None:
                continue
            ins = op.emit(eng)
            if op.dsem is not None:
                lst = ins if isinstance(ins, (list, tuple)) else [ins]
                assert len(lst) == op.n_dma, (len(lst), op.n_dma)
                for i_ in lst:
                    i_.then_inc(dsems[op.dsem], 16)
            elif op.inc:
                last = ins[-1] if isinstance(ins, (list, tuple)) else ins
                last.then_inc(esems[eng_name], 1)


def build_program(NT=NT, do_b=True, max_ops=None, marks=None):
    TOK = NT * 128
    nc = bass.Bass("TRN2", target_bir_lowering=False)

    def dram(name, shape, dt=F32, kind="ExternalInput"):
        return nc.dram_tensor(name, shape, dt, kind=kind).ap()

    x_d = dram("x", [TOK, D])
    w_in_d = dram("w_in", [D, INW])
    w_out_d = dram("w_out", [D, D])
    w1_d = dram("w_ff1", [D, DFF])
    w2_d = dram("w_ff2", [DFF, D])
    g1c_d = dram("g1c", [128, 8])
    goc_d = dram("goc", [128, 8])
    bsT_d = dram("bsT", [128, 4])
    wsT_d = dram("wsT", [128, 512])
    lng_d = dram("lng", [1, 512])
    lnb_d = dram("lnb", [1, 512])
    gq_d = dram("gq", [1, 64])
    gk_d = dram("gk", [1, 64])
    g2_d = dram("g2", [1, D])
    masks_d = dram("masks", [128, 16 * 128], BF16)
    ident_d = dram("ident", [128, 128], BF16)
    triu_d = dram("triu", [128, 128])
    y_d = dram("y", [TOK, D], kind="ExternalOutput")
    x1_d = y_d

    S = Sched()
    DS_C, DS_STG0, DS_XT0, DS_W1, DS_W2_0, DS_R0 = 0, 1, 4, 6, 7, 11
    N_DS = 17

    TOTAL_BYTES = 207 * 1024
    import contextlib
    with contextlib.ExitStack() as es:
        big = es.enter_context(nc.sbuf_tensor("big", [128, TOTAL_BYTES // 2], BF16))
        ps = [es.enter_context(nc.psum_tensor(f"ps{i}", [128, 512], F32)) for i in range(8)]
        esems = {e: es.enter_context(nc.semaphore(f"sem_{e}")) for e in ("pe", "act", "dve", "pool")}
        dsems = [es.enter_context(nc.semaphore(f"dsem{i}")) for i in range(N_DS)]
        block = es.enter_context(nc.Block())

        psf = [p[:] for p in ps]
        psb = [p[:].bitcast(BF16) for p in ps]

        def view(off, nbytes, dt):
            assert off % 64 == 0 and off + nbytes <= TOTAL_BYTES, (off, nbytes)
            v = big[:, off // 2:(off + nbytes) // 2]
            return v.bitcast(F32) if dt == F32 else v

        class Carve:
            def __init__(self, base, limit):
                self.o, self.limit = base, limit

            def take(self, nbytes, dt):
                nb = (nbytes + 63) // 64 * 64
                v = view(self.o, nbytes, dt)
                self.o += nb
                assert self.o <= self.limit, (self.o, self.limit)
                return v

        KB = 1024
        cp = Carve(0, 7 * KB)
        ident = cp.take(256, BF16)
        g2b = cp.take(4096, F32)
        neghalf = cp.take(64, F32)
        stF = [cp.take(512, F32) for _ in range(2)]
        stB = [cp.take(256, F32) for _ in range(2)]
        stP = cp.take(256, F32)
        g1c = cp.take(32, F32)
        goc = cp.take(32, F32)
        bsT = cp.take(16, F32)
        gqk = cp.take(256, F32)
        W1 = view(7 * KB, 64 * KB, BF16).rearrange("p (c f) -> p c f", c=8)
        stg = [view(7 * KB + i * 10240, 10240, F32) for i in range(3)]
        cs = Carve(7 * KB + 32 * KB, 71 * KB)
        wsT = cs.take(2048, F32)
        triu = cs.take(512, F32)
        gq_t = cs.take(256, F32)
        gk_t = cs.take(256, F32)
        ca = Carve(71 * KB, TOTAL_BYTES)
        w_in = ca.take(40 * KB, BF16).rearrange("p (c n) -> p c n", c=8)
        w_out = ca.take(16 * KB, BF16).rearrange("p (c n) -> p c n", c=8)
        KT = ca.take(16 * KB, BF16).rearrange("p (j t) -> p j t", j=4)
        VA = ca.take(16640, BF16).rearrange("p (b h e) -> p b h e", b=16, h=8)
        masks = ca.take(4096, BF16).rearrange("p (d q) -> p d q", d=16)
        lng = ca.take(2048, F32)
        lnb = ca.take(2048, F32)
        WmT = ca.take(1024, BF16).rearrange("p (g t) -> p g t", g=4)
        xt = [ca.take(4096, F32) for _ in range(2)]
        hb = ca.take(2048, BF16)
        qn = hb
        hT = ca.take(2048, BF16)
        sq = ca.take(4096, F32)
        tt = ca.take(2048, F32)
        y2 = ca.take(4096, F32)
        qT = [ca.take(1024, BF16).rearrange("p (j t) -> p j t", j=4) for _ in range(2)]
        gn = ca.take(1024, BF16)
        m2 = ca.take(2048, F32)
        mixa = ca.take(1024, BF16)
        mixm = [ca.take(1024, BF16) for _ in range(2)]
        PT = [ca.take(1024, BF16) for _ in range(4)]
        af = ca.take(2048, F32)
        mT = ca.take(2048, BF16)
        cb = Carve(71 * KB, TOTAL_BYTES)
        W2 = cb.take(64 * KB, BF16).rearrange("p (j d) -> p j d", j=32)
        fT = cb.take(32 * KB, BF16).rearrange("p (j t) -> p j t", j=32)
        xr = [cb.take(4096, F32) for _ in range(6)]
        h2b = cb.take(2048, BF16)
        h2T = cb.take(8192, BF16).rearrange("p (c t) -> p c t", c=8)
        rb = [cb.take(1024, BF16) for _ in range(2)]
        sqB = cb.take(4096, F32)

        def dma1(out, in_, **kw):
            return lambda e: e.dma_start(out=out, in_=in_, **kw)

        def rstd(src, dst, k, scale, eps, rsrc, rdst):
            S.add("pool", lambda e: e.tensor_scalar(out=dst, in0=src, scalar1=scale, scalar2=eps,
                                                    op0=ALU.mult, op1=ALU.add),
                  reads=[rsrc], writes=[rdst])
            S.add("pool", lambda e: e.tensor_tensor(out=dst, in0=dst, in1=neghalf[:, 0:k], op=ALU.pow),
                  reads=[rdst], writes=[rdst])

        def transposes8(src, bank, rsrcs):
            def emit(e):
                last = None
                for c in range(8):
                    last = e.transpose(psb[bank][:, c * 128:(c + 1) * 128], src[:, c * 128:(c + 1) * 128], ident)
                return last
            S.add("pe", emit, reads=list(rsrcs) + ["ident"], writes=[f"ps{bank}"])

        def consts(e):
            return [
                e.dma_start(out=ident, in_=ident_d),
                e.dma_start(out=masks.rearrange("p d q -> p (d q)"), in_=masks_d),
                e.dma_start(out=triu, in_=triu_d),
                e.dma_start(out=g1c, in_=g1c_d),
                e.dma_start(out=goc, in_=goc_d),
                e.dma_start(out=bsT, in_=bsT_d),
                e.dma_start(out=wsT, in_=wsT_d),
                e.dma_start(out=lng, in_=lng_d.partition_broadcast(128)),
                e.dma_start(out=lnb, in_=lnb_d.partition_broadcast(128)),
                e.dma_start(out=gq_t, in_=gq_d.partition_broadcast(128)),
                e.dma_start(out=gk_t, in_=gk_d.partition_broadcast(128)),
                e.dma_start(out=g2b, in_=g2_d.partition_broadcast(128)),
            ]
        S.add("sp", consts, writes=["ident", "masks", "triu", "g1c", "goc", "bsT", "wsT", "lng", "lnb",
                                    "gq", "gk", "g2b"], dsem=DS_C, n_dma=12)
        S.add("pool", lambda e: e.memset(neghalf, -0.5), writes=["neghalf"])
        S.add("pool", lambda e: e.memset(VA[:, :, :, 64:65], 1.0), writes=["VAones"])
        S.add("dve", lambda e: e.tensor_tensor(out=gqk, in0=gq_t, in1=gk_t, op=ALU.mult),
              reads=["gq", "gk"], writes=["gqk"])
        S.add("dve", lambda e: e.tensor_tensor(
            out=WmT, in0=wsT.rearrange("p (g t) -> p g t", g=4),
            in1=triu.unsqueeze(1).broadcast_to([128, 4, 128]), op=ALU.mult),
            reads=["wsT", "triu"], writes=["WmT"])

        k = 0
        for c in range(8):
            s = k % 3
            k += 1
            S.add("sp", dma1(stg[s][:, 0:INW], w_in_d[c * 128:(c + 1) * 128, :]),
                  writes=[f"stg{s}"], dsem=DS_STG0 + s)
            if c % 2 == 0:
                S.add("act", lambda e, c=c, s=s: e.activation(out=w_in[:, c, :], in_=stg[s][:, 0:INW],
                                                              func=AF.Copy, scale=g1c[:, c:c + 1]),
                      reads=[f"stg{s}", "g1c"], writes=["w_in"])
            else:
                S.add("dve", lambda e, c=c, s=s: e.tensor_scalar(out=w_in[:, c, :], in0=stg[s][:, 0:INW],
                                                                 scalar1=g1c[:, c:c + 1], scalar2=None,
                                                                 op0=ALU.mult),
                      reads=[f"stg{s}", "g1c"], writes=["w_in"])
        for c in range(8):
            s = k % 3
            k += 1
            S.add("sp", dma1(stg[s][:, 0:D], w_out_d[c * 128:(c + 1) * 128, :]),
                  writes=[f"stg{s}"], dsem=DS_STG0 + s)
            if c % 2 == 0:
                S.add("act", lambda e, c=c, s=s: e.activation(out=w_out[:, c, :], in_=stg[s][:, 0:D],
                                                              func=AF.Copy, scale=goc[:, c:c + 1]),
                      reads=[f"stg{s}", "goc"], writes=["w_out"])
            else:
                S.add("dve", lambda e, c=c, s=s: e.tensor_scalar(out=w_out[:, c, :], in0=stg[s][:, 0:D],
                                                                 scalar1=goc[:, c:c + 1], scalar2=None,
                                                                 op0=ALU.mult),
                      reads=[f"stg{s}", "goc"], writes=["w_out"])

        def w1_load(e):
            return [e.dma_start(out=W1[:, c, hh * 2048:(hh + 1) * 2048],
                                in_=w1_d[c * 128:(c + 1) * 128, hh * 2048:(hh + 1) * 2048])
                    for c in range(8) for hh in range(2)]
        S.add("pool", w1_load, writes=["W1", "stg0", "stg1", "stg2", "wsT", "triu", "gq", "gk"],
              dsem=DS_W1, n_dma=16)

        fbanks = [0, 1, 5]
        fb_i = [0]

        def next_fb():
            b = fbanks[fb_i[0] % 3]
            fb_i[0] += 1
            return b

        def inproj_block(blk, bank):
            def emit(e):
                last = None
                for c in range(8):
                    last = e.matmul(psf[bank], lhsT=hT[:, c * 128:(c + 1) * 128],
                                    rhs=w_in[:, c, blk * 512:(blk + 1) * 512],
                                    start=(c == 0), stop=(c == 7))
                return last
            S.add("pe", emit, reads=["hT", "w_in"], writes=[f"ps{bank}"])

        def gelu_chain(bank, half, ydst_name):
            sl = slice(half * 512, (half + 1) * 512)
            S.add("act", lambda e: e.activation(out=sq[:, sl], in_=psf[bank], func=AF.Square,
                                                scale=float(np.sqrt(GA))),
                  reads=[f"ps{bank}"], writes=[f"sq{half}"])
            S.add("dve", lambda e: e.scalar_tensor_tensor(out=tt, in0=sq[:, sl], scalar=1.0,
                                                          in1=psf[bank], op0=ALU.add, op1=ALU.mult),
                  reads=[f"sq{half}", f"ps{bank}"], writes=["tt"])
            S.add("act", lambda e: e.activation(out=tt, in_=tt, func=AF.Tanh, scale=GC),
                  reads=["tt"], writes=["tt"])
            S.add("dve", lambda e: e.scalar_tensor_tensor(out=y2[:, sl], in0=tt, scalar=1.0,
                                                          in1=psf[bank], op0=ALU.add, op1=ALU.mult),
                  reads=["tt", f"ps{bank}"], writes=[ydst_name])

        def front(t):
            p = t % 2
            b = t % 16
            st = stF[p]
            S.add("sp", dma1(xt[p], x_d[t * 128:(t + 1) * 128, :]), writes=[f"xt{p}"], dsem=DS_XT0 + p)
            yield
            S.add("act", lambda e: e.activation(out=sq, in_=xt[p], func=AF.Square, accum_out=st[:, 0:1]),
                  reads=[f"xt{p}"], writes=["sq0", "sq1", f"ss1_{p}"])
            rstd(st[:, 0:1], st[:, 1:2], 1, 1.0 / D, EPS, f"ss1_{p}", f"r1_{p}")
            S.add("dve", lambda e: e.tensor_scalar(out=hb, in0=xt[p], scalar1=st[:, 1:2], scalar2=None,
                                                   op0=ALU.mult),
                  reads=[f"xt{p}", f"r1_{p}"], writes=["hb0", "hb1"])
            yield
            bk = next_fb()
            transposes8(hb, bk, ["hb0", "hb1"])
            S.add("act", lambda e: e.copy(out=hT, in_=psb[bk]), reads=[f"ps{bk}"], writes=["hT"])
            yield
            bq = next_fb()
            inproj_block(0, bq)
            bkk = next_fb()
            inproj_block(1, bkk)
            yield
            S.add("act", lambda e: e.activation(out=sq[:, 0:512], in_=psf[bq], func=AF.Square),
                  reads=[f"ps{bq}"], writes=["sq0"])
            S.add("act", lambda e: e.activation(out=sq[:, 512:1024], in_=psf[bkk], func=AF.Square),
                  reads=[f"ps{bkk}"], writes=["sq1"])
            S.add("dve", lambda e: e.tensor_reduce(out=st[:, 8:24],
                                                   in_=sq.rearrange("p (h d) -> p h d", d=64),
                                                   axis=AX.X, op=ALU.add),
                  reads=["sq0", "sq1"], writes=[f"ssqk_{p}"])
            rstd(st[:, 8:24], st[:, 24:40], 16, 1.0 / 64, EPS, f"ssqk_{p}", f"rqk_{p}")
            S.add("dve", lambda e: e.tensor_tensor(
                out=qn[:, 0:512].rearrange("p (h d) -> p h d", d=64),
                in0=psf[bq].rearrange("p (h d) -> p h d", d=64),
                in1=st[:, 24:32].unsqueeze(2).broadcast_to([128, 8, 64]), op=ALU.mult),
                reads=[f"ps{bq}", f"rqk_{p}"], writes=["hb0"])
            S.add("dve", lambda e: e.tensor_tensor(
                out=tt.rearrange("p (h d) -> p h d", d=64),
                in0=psf[bkk].rearrange("p (h d) -> p h d", d=64),
                in1=st[:, 32:40].unsqueeze(2).broadcast_to([128, 8, 64]), op=ALU.mult),
                reads=[f"ps{bkk}", f"rqk_{p}"], writes=["tt"])
            S.add("pool", lambda e: e.tensor_tensor(
                out=qn[:, 512:1024].rearrange("p (h d) -> p h d", d=64),
                in0=tt.rearrange("p (h d) -> p h d", d=64),
                in1=gqk.unsqueeze(1).broadcast_to([128, 8, 64]), op=ALU.mult),
                reads=["tt", "gqk"], writes=["hb1"])
            yield
            bv = next_fb()
            inproj_block(2, bv)
            S.add("act", lambda e: e.copy(out=VA[:, b, :, 0:64],
                                          in_=psf[bv].rearrange("p (h d) -> p h d", d=64)),
                  reads=[f"ps{bv}", "VAones"], writes=[f"VA{b}"])
            yield
            bt = next_fb()
            transposes8(qn, bt, ["hb0", "hb1"])
            S.add("act", lambda e: e.copy(out=qT[p].rearrange("p j t -> p (j t)"), in_=psb[bt][:, 0:512]),
                  reads=[f"ps{bt}"], writes=[f"qT{p}"])
            S.add("dve", lambda e: e.tensor_copy(
                out=KT[:, :, b * 128:(b + 1) * 128],
                in_=psb[bt][:, 512:1024].rearrange("p (j t) -> p j t", j=4)),
                reads=[f"ps{bt}"], writes=[f"KT{b}"])
            yield
            bu = next_fb()
            inproj_block(3, bu)
            gelu_chain(bu, 0, "y2u")
            yield
            bg = next_fb()
            inproj_block(4, bg)
            gelu_chain(bg, 1, "y2g")
            yield
            y2g = y2[:, 512:1024].rearrange("p (g c) -> p g c", g=4)
            bst = st[:, 40:64].rearrange("p (g s) -> p g s", g=4)

            def bn_s(e):
                last = None
                for g_ in range(4):
                    last = e.bn_stats(out=bst[:, g_, :], in_=y2g[:, g_, :])
                return last
            S.add("dve", bn_s, reads=["y2g"], writes=[f"bst_{p}"])
            mv = st[:, 64:72].rearrange("p (g s) -> p g s", g=4)

            def bn_a(e):
                last = None
                for g_ in range(4):
                    last = e.bn_aggr(out=mv[:, g_, :], in_=bst[:, g_, :])
                return last
            S.add("dve", bn_a, reads=[f"bst_{p}"], writes=[f"mv_{p}"])
            rg = st[:, 72:76]
            rstd(mv[:, :, 1], rg, 4, 1.0, 4 * EPS, f"mv_{p}", f"rg_{p}")
            gnf = sq[:, 0:512].rearrange("p (g c) -> p g c", g=4)

            def ln_apply(e):
                last = None
                for g_ in range(4):
                    last = e.tensor_scalar(out=gnf[:, g_, :], in0=y2g[:, g_, :], scalar1=mv[:, g_, 0:1],
                                           scalar2=rg[:, g_:g_ + 1], op0=ALU.subtract, op1=ALU.mult)
                return last
            S.add("dve", ln_apply, reads=["y2g", f"mv_{p}", f"rg_{p}"], writes=["sq0"])
            S.add("pool", lambda e: e.tensor_tensor(out=sq[:, 0:512], in0=sq[:, 0:512], in1=lng, op=ALU.mult),
                  reads=["sq0", "lng"], writes=["sq0"])
            S.add("dve", lambda e: e.tensor_tensor(out=gn, in0=sq[:, 0:512], in1=lnb, op=ALU.add),
                  reads=["sq0", "lnb"], writes=["gn"])
            yield
            bz = next_fb()

            def spat(e):
                last = None
                for g_ in range(4):
                    last = e.matmul(psf[bz][:, g_ * 128:(g_ + 1) * 128], lhsT=WmT[:, g_, :],
                                    rhs=gn[:, g_ * 128:(g_ + 1) * 128], start=True, stop=True,
                                    skip_group_check=True)
                return last
            S.add("pe", spat, reads=["gn", "WmT"], writes=[f"ps{bz}"])

            def gate(e):
                last = None
                for g_ in range(4):
                    sl = slice(g_ * 128, (g_ + 1) * 128)
                    last = e.scalar_tensor_tensor(out=m2[:, sl], in0=psf[bz][:, sl], scalar=bsT[:, g_:g_ + 1],
                                                  in1=y2[:, sl], op0=ALU.add, op1=ALU.mult)
                return last
            S.add("dve", gate, reads=[f"ps{bz}", "bsT", "y2u"], writes=["m2"])
            S.add("act", lambda e: e.activation(out=sq[:, 512:1024], in_=m2, func=AF.Square,
                                                accum_out=st[:, 2:3]),
                  reads=["m2"], writes=["sq1", f"ssm_{p}"])
            rstd(st[:, 2:3], st[:, 3:4], 1, 1.0 / 512, 4 * EPS, f"ssm_{p}", f"rm_{p}")
            S.add("dve", lambda e: e.tensor_scalar(out=mixm[p], in0=m2, scalar1=st[:, 3:4],
                                                   scalar2=None, op0=ALU.mult),
                  reads=["m2", f"rm_{p}"], writes=[f"mixm{p}"])

        out_ops = []
        sbanks = [2, 3, 4]
        sb_i = [0]
        pt_i = [0]

        def back(t):
            p = t % 2
            b = t % 16
            st = stB[p]
            pend = []

            def unit_qk(kb, par):
                bank = sbanks[sb_i[0] % 3]
                sb_i[0] += 1
                slot = pt_i[0] % 4
                pt_i[0] += 1

                def emit(e):
                    last = None
                    for j in range(4):
                        rows = slice(par * 64, par * 64 + 64)
                        last = e.matmul(psf[bank][:, j * 128:(j + 1) * 128],
                                        lhsT=KT[rows, j, kb * 128:(kb + 1) * 128],
                                        rhs=qT[p][rows, j, :], start=True, stop=True, skip_group_check=True)
                    return last
                S.add("pe", emit, reads=[f"KT{kb}", f"qT{p}"], writes=[f"ps{bank}"])
                S.add("act", lambda e: e.activation(out=PT[slot], in_=psf[bank], func=AF.Exp, scale=0.125),
                      reads=[f"ps{bank}"], writes=[f"PT{slot}"])
                dd = b - kb
                S.add("dve", lambda e: e.tensor_tensor(
                    out=PT[slot].rearrange("p (j q) -> p j q", j=4),
                    in0=PT[slot].rearrange("p (j q) -> p j q", j=4),
                    in1=masks[:, dd, :].unsqueeze(1).broadcast_to([128, 4, 128]), op=ALU.mult),
                    reads=[f"PT{slot}", "masks"], writes=[f"PT{slot}"])
                return slot

            def unit_pv(kb, par, slot):
                def emit(e):
                    last = None
                    for j in range(4):
                        h = 2 * j + par
                        bank = 6 if h < 4 else 7
                        col = (h % 4) * 65
                        first = (kb == 0 and par == 0 and (h == 0 or h == 4))
                        last = e.matmul(psf[bank][:, col:col + 65], lhsT=PT[slot][:, j * 128:(j + 1) * 128],
                                        rhs=VA[:, kb, h, :], start=first, stop=(kb == b),
                                        skip_group_check=True)
                    return last
                S.add("pe", emit, reads=[f"PT{slot}", f"VA{kb}"], writes=["ps6", "ps7"])

            for kb in range(b + 1):
                cur = []
                for par in range(2):
                    cur.append((kb, par, unit_qk(kb, par)))
                for (k_, p_, s_) in pend:
                    unit_pv(k_, p_, s_)
                pend = cur
                yield
            for (k_, p_, s_) in pend:
                unit_pv(k_, p_, s_)
            rden = st[:, 0:8]

            def rd(e):
                i0 = e.reciprocal(out=rden[:, 0:4].unsqueeze(2),
                                  in_=psf[6][:, 0:260].rearrange("p (h e) -> p h e", e=65)[:, :, 64:65])
                i1 = e.reciprocal(out=rden[:, 4:8].unsqueeze(2),
                                  in_=psf[7][:, 0:260].rearrange("p (h e) -> p h e", e=65)[:, :, 64:65])
                return i1
            S.add("dve", rd, reads=["ps6", "ps7"], writes=[f"rden_{p}"])

            def an(e):
                last = None
                for hb_, bank in ((0, 6), (1, 7)):
                    last = e.tensor_tensor(
                        out=af[:, hb_ * 256:(hb_ + 1) * 256].rearrange("p (h d) -> p h d", d=64),
                        in0=psf[bank][:, 0:260].rearrange("p (h e) -> p h e", e=65)[:, :, 0:64],
                        in1=rden[:, hb_ * 4:(hb_ + 1) * 4].unsqueeze(2).broadcast_to([128, 4, 64]),
                        op=ALU.mult)
                return last
            S.add("dve", an, reads=["ps6", "ps7", f"rden_{p}"], writes=["af"])
            S.add("act", lambda e: e.activation(out=mT[:, 0:512], in_=af, func=AF.Square, accum_out=st[:, 8:9]),
                  reads=["af"], writes=["mTjunk", f"ssa_{p}"])
            rstd(st[:, 8:9], st[:, 9:10], 1, 1.0 / 512, EPS, f"ssa_{p}", f"ra_{p}")
            S.add("dve", lambda e: e.tensor_scalar(out=mixa, in0=af, scalar1=st[:, 9:10],
                                                   scalar2=None, op0=ALU.mult),
                  reads=["af", f"ra_{p}"], writes=["mixa"])
            yield
            bank = sbanks[sb_i[0] % 3]
            sb_i[0] += 1

            def tr(e):
                last = None
                for c in range(8):
                    src = mixa[:, c * 128:(c + 1) * 128] if c < 4 else mixm[p][:, (c - 4) * 128:(c - 3) * 128]
                    last = e.transpose(psb[bank][:, c * 128:(c + 1) * 128], src, ident)
                return last
            S.add("pe", tr, reads=["mixa", f"mixm{p}", "ident"], writes=[f"ps{bank}"])
            S.add("act", lambda e: e.copy(out=mT, in_=psb[bank]), reads=[f"ps{bank}"], writes=["mT", "mTjunk"])
            yield
            for blk in range(2):
                bank2 = sbanks[sb_i[0] % 3]
                sb_i[0] += 1

                def op_(e, blk=blk, bank2=bank2):
                    last = None
                    for c in range(8):
                        last = e.matmul(psf[bank2], lhsT=mT[:, c * 128:(c + 1) * 128],
                                        rhs=w_out[:, c, blk * 512:(blk + 1) * 512],
                                        start=(c == 0), stop=(c == 7))
                    return last
                S.add("pe", op_, reads=["mT", "w_out"], writes=[f"ps{bank2}"])
                S.add("dve", lambda e, blk=blk, bank2=bank2: e.tensor_tensor(
                    out=xt[p][:, blk * 512:(blk + 1) * 512], in0=psf[bank2],
                    in1=xt[p][:, blk * 512:(blk + 1) * 512], op=ALU.add),
                    reads=[f"ps{bank2}", f"xt{p}"], writes=[f"xt{p}"])
            dst_ = x1_d if do_b else y_d
            o_ = S.add("sp", dma1(dst_[t * 128:(t + 1) * 128, :], xt[p]), reads=[f"xt{p}"], writes=[f"x1d{t}"],
                       dsem=DS_XT0 + p)
            if not do_b:
                out_ops.append(o_)

        def mark(name):
            if marks is not None:
                marks.append((name, len(S.ops)))

        def run(gen):
            for _ in gen:
                mark("step")

        def interleave(g1, g2):
            gens = [g for g in (g1, g2) if g is not None]
            while gens:
                for g in list(gens):
                    try:
                        next(g)
                    except StopIteration:
                        gens.remove(g)

        mark("prologue_end")
        run(front(0))
        mark("front0_end")
        for t in range(NT):
            nxt = t + 1
            if nxt < NT and nxt % 16 != 0:
                interleave(back(t), front(nxt))
            else:
                run(back(t))
                if nxt < NT:
                    run(front(nxt))
            mark(f"tile{t}_end")

        mark("phaseA_end")
        S.barrier()
        for q4 in range(4 if do_b else 0):
            S.add("pool", lambda e, q4=q4: [e.dma_start(
                out=W2[:, q4 * 8 + jj, :], in_=w2_d[(q4 * 8 + jj) * 128:(q4 * 8 + jj + 1) * 128, :])
                for jj in range(8)], writes=[f"W2_{q4}"], dsem=DS_W2_0 + q4, n_dma=8)

        ring_i = [0]
        f1banks = [0, 1, 2, 3]
        f2banks = [4, 5]
        trbanks = [6, 7]
        cnt1 = [0]
        cnt2 = [0]
        cntt = [0]
        for gi in range(NT // 4 if do_b else 0):
            slots = []
            for i in range(4):
                t = gi * 4 + i
                r = ring_i[0] % 6
                ring_i[0] += 1
                slots.append(r)
                S.add("sp", dma1(xr[r], x1_d[t * 128:(t + 1) * 128, :]), reads=[f"x1d{t}"], writes=[f"xr{r}"],
                      dsem=DS_R0 + r)
                S.add("act", lambda e, r=r: e.activation(out=sqB, in_=xr[r], func=AF.Square,
                                                         accum_out=stP[:, 0:1]),
                      reads=[f"xr{r}"], writes=["sqB", "ss2"])
                rstd(stP[:, 0:1], stP[:, 1:2], 1, 1.0 / D, EPS, "ss2", "r2")
                S.add("dve", lambda e, r=r: e.scalar_tensor_tensor(out=h2b, in0=xr[r], scalar=stP[:, 1:2],
                                                                   in1=g2b, op0=ALU.mult, op1=ALU.mult),
                      reads=[f"xr{r}", "r2", "g2b"], writes=["h2b"])
                bank = trbanks[cntt[0] % 2]
                cntt[0] += 1
                transposes8(h2b, bank, ["h2b"])
                S.add("act", lambda e, i=i, bank=bank: e.copy(
                    out=h2T[:, :, i * 128:(i + 1) * 128],
                    in_=psb[bank].rearrange("p (c t) -> p c t", c=8)),
                    reads=[f"ps{bank}"], writes=["h2T"])
            for j in range(32):
                bank = f1banks[cnt1[0] % 4]
                cnt1[0] += 1
                rslot = j % 2

                def f1(e, j=j, bank=bank):
                    last = None
                    for c in range(8):
                        last = e.matmul(psf[bank], lhsT=W1[:, c, j * 128:(j + 1) * 128], rhs=h2T[:, c, :],
                                        start=(c == 0), stop=(c == 7))
                    return last
                S.add("pe", f1, reads=["W1", "h2T"], writes=[f"ps{bank}"])
                S.add("act", lambda e, bank=bank, rslot=rslot: e.activation(out=rb[rslot], in_=psf[bank],
                                                                            func=AF.Relu),
                      reads=[f"ps{bank}"], writes=[f"rb{rslot}"])
                S.add("dve", lambda e, j=j, bank=bank, rslot=rslot: e.scalar_tensor_tensor(
                    out=fT[:, j, :], in0=psf[bank], scalar=0.0, in1=rb[rslot], op0=ALU.max, op1=ALU.mult),
                    reads=[f"ps{bank}", f"rb{rslot}"], writes=[f"fT{j}"])
            for i in range(4):
                t = gi * 4 + i
                r = slots[i]
                for blk in range(2):
                    bank = f2banks[cnt2[0] % 2]
                    cnt2[0] += 1

                    def f2(e, i=i, blk=blk, bank=bank):
                        last = None
                        for j in range(32):
                            last = e.matmul(psf[bank], lhsT=fT[:, j, i * 128:(i + 1) * 128],
                                            rhs=W2[:, j, blk * 512:(blk + 1) * 512],
                                            start=(j == 0), stop=(j == 31))
                        return last
                    S.add("pe", f2, reads=[f"fT{j}" for j in range(32)] + [f"W2_{q}" for q in range(4)],
                          writes=[f"ps{bank}"])
                    S.add("dve", lambda e, r=r, blk=blk, bank=bank: e.tensor_tensor(
                        out=xr[r][:, blk * 512:(blk + 1) * 512], in0=psf[bank],
                        in1=xr[r][:, blk * 512:(blk + 1) * 512], op=ALU.add),
                        reads=[f"ps{bank}", f"xr{r}"], writes=[f"xr{r}"])
                out_ops.append(S.add("sp", dma1(y_d[t * 128:(t + 1) * 128, :], xr[r]), reads=[f"xr{r}"],
                                     writes=[f"y{t}"], dsem=DS_R0 + r))
        S.add("sp", None, extra_deps=out_ops)

        if max_ops is not None:
            S.ops = S.ops[:max_ops]
            lasts = {}
            for o in S.ops:
                lasts[S._key(o)] = o
            S.add("sp", None, extra_deps=[o for o in lasts.values() if o.emit is not None])
        S.finalize()

        @block.sync
        def _(e):
            S.emit_engine("sp", e, esems, dsems)

        @block.tensor
        def _(e):
            S.emit_engine("pe", e, esems, dsems)

        @block.scalar
        def _(e):
            S.emit_engine("act", e, esems, dsems)

        @block.vector
        def _(e):
            S.emit_engine("dve", e, esems, dsems)

        @block.gpsimd
        def _(e):
            S.emit_engine("pool", e, esems, dsems)

    return nc


_CONST_CACHE = {}


def _consts():
    if "c" not in _CONST_CACHE:
        kl = np.arange(128)[:, None]
        ql = np.arange(128)[None, :]
        masks = np.zeros((128, 16, 128), np.float32)
        for dd in range(16):
            dist = 128 * dd + ql - kl
            ge = dist >= 0
            m = (ge & (dist <= 128)).astype(np.float32)
            m += (ge & (dist % 4 == 0) & (dist <= 512)).astype(np.float32)
            m += (ge & (dist % 16 == 0) & (dist <= 2048)).astype(np.float32)
            masks[:, dd, :] = m
        masks = masks.reshape(128, 16 * 128).astype(ml_dtypes.bfloat16)
        ident = np.eye(128, dtype=np.float32).astype(ml_dtypes.bfloat16)
        triu = (kl <= ql).astype(np.float32)
        _CONST_CACHE["c"] = (masks, ident, triu)
    return _CONST_CACHE["c"]


def _prep(x, norm1_g, w_in, q_norm_g, k_norm_g, ln_v_g, ln_v_b, w_spatial, b_spatial, attn_out_g,
          gmlp_out_g, w_out, norm2_g, w_ff1, w_ff2):
    f = lambda a: np.ascontiguousarray(np.asarray(a, dtype=np.float32))
    x2 = f(x).reshape(-1, D)
    masks, ident, triu = _consts()
    shared = {
        "w_in": f(w_in)[0], "w_out": f(w_out)[0], "w_ff1": f(w_ff1)[0], "w_ff2": f(w_ff2)[0],
        "g1c": np.ascontiguousarray(f(norm1_g)[0].reshape(8, 128).T),
        "goc": np.ascontiguousarray(np.concatenate([f(attn_out_g)[0], f(gmlp_out_g)[0]]).reshape(8, 128).T),
        "bsT": np.ascontiguousarray(f(b_spatial)[0].T),
        "wsT": np.ascontiguousarray(f(w_spatial)[0].transpose(2, 0, 1).reshape(128, 512)),
        "lng": f(ln_v_g)[0].reshape(1, 512), "lnb": f(ln_v_b)[0].reshape(1, 512),
        "gq": f(q_norm_g)[0].reshape(1, 64), "gk": f(k_norm_g)[0].reshape(1, 64),
        "g2": f(norm2_g)[0].reshape(1, D),
        "masks": masks, "ident": ident, "triu": triu,
    }
    return x2, shared


def kernel(x, norm1_g, w_in, q_norm_g, k_norm_g, ln_v_g, ln_v_b, w_spatial, b_spatial, attn_out_g,
           gmlp_out_g, w_out, norm2_g, w_ff1, w_ff2):
    x2, shared = _prep(x, norm1_g, w_in, q_norm_g, k_norm_g, ln_v_g, ln_v_b, w_spatial, b_spatial,
                       attn_out_g, gmlp_out_g, w_out, norm2_g, w_ff1, w_ff2)
    nc = build_program()
    in_maps = []
    for c in range(NCORES):
        m = dict(shared)
        m["x"] = x2[c * TOK:(c + 1) * TOK]
        in_maps.append(m)
    res = run_bass_kernel_spmd(nc, in_maps, core_ids=list(range(NCORES)))
    out = np.concatenate([np.asarray(r["y"], dtype=np.float32) for r in res.results], axis=0)
    return out.reshape(16, 2048, D)
```

```python
import numpy as np
import ml_dtypes
import concourse.bass as bass
import concourse.mybir as mybir
from concourse.bass_utils import run_bass_kernel_spmd

F32 = mybir.dt.float32
BF16 = mybir.dt.bfloat16
AF = mybir.ActivationFunctionType
ALU = mybir.AluOpType
AX = mybir.AxisListType

NCORES = 8
TOK = 4096
D = 1024
NT = TOK // 128
DFF = 4096
INW = 2560
EPS = 1e-6
GC = 0.7978845608028654
GA = 0.044715


class _Op:
    __slots__ = ("eng", "emit", "deps", "inc", "ticket", "dsem", "n_dma")


class Sched:
    ENGS = ("pe", "act", "dve", "pool", "sp")

    def __init__(self):
        self.ops = []
        self.lastw = {}
        self.readers = {}
        self.last_on = {}

    @staticmethod
    def _key(op):
        return ("d", op.dsem) if op.dsem is not None else ("e", op.eng)

    def add(self, eng, emit, reads=(), writes=(), dsem=None, n_dma=1, extra_deps=()):
        op = _Op()
        op.eng, op.emit, op.dsem, op.n_dma = eng, emit, dsem, n_dma
        op.inc = dsem is not None
        op.ticket = None
        deps = set(extra_deps)
        writes = list(writes) + [r for r in reads if r.startswith("ps")]
        reads = [r for r in reads if not r.startswith("ps")]
        for r in reads:
            w = self.lastw.get(r)
            if w is not None:
                deps.add(w)
        for w_ in writes:
            w = self.lastw.get(w_)
            if w is not None:
                deps.add(w)
            for rd in self.readers.get(w_, {}).values():
                deps.add(rd)
        op.deps = [d for d in deps
                   if not (d.eng == "pe" and eng == "pe" and d.dsem is None and dsem is None)]
        for d in op.deps:
            d.inc = True
        for r in reads:
            self.readers.setdefault(r, {})[self._key(op)] = op
        for w_ in writes:
            self.lastw[w_] = op
            self.readers[w_] = {}
        self.ops.append(op)
        self.last_on[self._key(op)] = op
        return op

    def barrier(self):
        lasts = list(self.last_on.values())
        for e in self.ENGS:
            self.add(e, None, extra_deps=[l for l in lasts])

    def finalize(self):
        cnt = {}
        for op in self.ops:
            k = self._key(op)
            if op.dsem is not None:
                cnt[k] = cnt.get(k, 0) + op.n_dma
                op.ticket = cnt[k]
            elif op.inc and op.emit is not None:
                cnt[k] = cnt.get(k, 0) + 1
                op.ticket = cnt[k]
            else:
                op.ticket = cnt.get(k, 0)
        return cnt

    def emit_engine(self, eng_name, eng, esems, dsems):
        waited = {}
        for op in self.ops:
            if op.eng != eng_name:
                continue
            need = {}
            for d in op.deps:
                k = self._key(d)
                v = d.ticket * 16 if d.dsem is not None else d.ticket
                if v > need.get(k, 0):
                    need[k] = v
            for k, v in need.items():
                if v <= 0:
                    continue
                if waited.get(k, 0) < v:
                    sem = dsems[k[1]] if k[0] == "d" else esems[k[1]]
                    eng.wait_ge(sem, v)
                    waited[k] = v
            if op.emit is None:
                continue
            ins = op.emit(eng)
            if op.dsem is not None:
                lst = ins if isinstance(ins, (list, tuple)) else [ins]
                assert len(lst) == op.n_dma, (len(lst), op.n_dma)
                for i_ in lst:
                    i_.then_inc(dsems[op.dsem], 16)
            elif op.inc:
                last = ins[-1] if isinstance(ins, (list, tuple)) else ins
                last.then_inc(esems[eng_name], 1)


def build_program(NT=NT, do_b=True, max_ops=None, marks=None):
    TOK = NT * 128
    nc = bass.Bass("TRN2", target_bir_lowering=False)

    def dram(name, shape, dt=F32, kind="ExternalInput"):
        return nc.dram_tensor(name, shape, dt, kind=kind).ap()

    x_d = dram("x", [TOK, D])
    w_in_d = dram("w_in", [D, INW])
    w_out_d = dram("w_out", [D, D])
    w1_d = dram("w_ff1", [D, DFF])
    w2_d = dram("w_ff2", [DFF, D])
    g1c_d = dram("g1c", [128, 8])
    goc_d = dram("goc", [128, 8])
    bsT_d = dram("bsT", [128, 4])
    wsT_d = dram("wsT", [128, 512])
    lng_d = dram("lng", [1, 512])
    lnb_d = dram("lnb", [1, 512])
    gq_d = dram("gqc", [128, 1])
    gk_d = dram("gkc", [128, 1])
    g2_d = dram("g2", [1, D])
    masks_d = dram("masks", [128, 16 * 128], BF16)
    ident_d = dram("ident", [128, 128], BF16)
    triu_d = dram("triu", [128, 128])
    y_d = dram("y", [TOK, D], kind="ExternalOutput")
    x1_d = y_d

    S = Sched()
    DS_C, DS_STG0, DS_XT0, DS_W1, DS_W2_0, DS_R0 = 0, 1, 4, 6, 7, 11
    N_DS = 17

    TOTAL_BYTES = 207 * 1024
    import contextlib
    with contextlib.ExitStack() as es:
        big = es.enter_context(nc.sbuf_tensor("big", [128, TOTAL_BYTES // 2], BF16))
        ps = [es.enter_context(nc.psum_tensor(f"ps{i}", [128, 512], F32)) for i in range(8)]
        esems = {e: es.enter_context(nc.semaphore(f"sem_{e}")) for e in ("pe", "act", "dve", "pool")}
        dsems = [es.enter_context(nc.semaphore(f"dsem{i}")) for i in range(N_DS)]
        block = es.enter_context(nc.Block())

        psf = [p[:] for p in ps]
        psb = [p[:].bitcast(BF16) for p in ps]

        def view(off, nbytes, dt):
            assert off % 64 == 0 and off + nbytes <= TOTAL_BYTES, (off, nbytes)
            v = big[:, off // 2:(off + nbytes) // 2]
            return v.bitcast(F32) if dt == F32 else v

        class Carve:
            def __init__(self, base, limit):
                self.o, self.limit = base, limit

            def take(self, nbytes, dt):
                nb = (nbytes + 63) // 64 * 64
                v = view(self.o, nbytes, dt)
                self.o += nb
                assert self.o <= self.limit, (self.o, self.limit)
                return v

        KB = 1024
        cp = Carve(0, 7 * KB)
        ident = cp.take(256, BF16)
        g2b = cp.take(4096, F32)
        neghalf = cp.take(64, F32)
        stF = [cp.take(512, F32) for _ in range(2)]
        stB = [cp.take(256, F32) for _ in range(2)]
        stP = cp.take(256, F32)
        g1c = cp.take(32, F32)
        goc = cp.take(32, F32)
        bsT = cp.take(16, F32)
        gqkT = cp.take(64, F32)
        W1 = view(7 * KB, 64 * KB, BF16).rearrange("p (c f) -> p c f", c=8)
        stg = [view(7 * KB + i * 10240, 10240, F32) for i in range(3)]
        cs = Carve(7 * KB + 32 * KB, 71 * KB)
        wsT = cs.take(2048, F32)
        triu = cs.take(512, F32)
        gq_t = cs.take(64, F32)
        gk_t = cs.take(64, F32)
        ca = Carve(71 * KB, TOTAL_BYTES)
        w_in = ca.take(40 * KB, BF16).rearrange("p (c n) -> p c n", c=8)
        w_out = ca.take(16 * KB, BF16).rearrange("p (c n) -> p c n", c=8)
        KT = ca.take(16 * KB, BF16).rearrange("p (j t) -> p j t", j=4)
        VA = ca.take(16640, BF16).rearrange("p (b h e) -> p b h e", b=16, h=8)
        masks = ca.take(4096, BF16).rearrange("p (d q) -> p d q", d=16)
        lng = ca.take(2048, F32)
        lnb = ca.take(2048, F32)
        WmT = ca.take(1024, BF16).rearrange("p (g t) -> p g t", g=4)
        xt = [ca.take(4096, F32) for _ in range(2)]
        hb = ca.take(2048, BF16)
        qn = hb
        hT = ca.take(2048, BF16)
        sq = ca.take(4096, F32)
        tt = ca.take(2048, F32)
        y2 = ca.take(4096, F32)
        qT = [ca.take(1024, BF16).rearrange("p (j t) -> p j t", j=4) for _ in range(2)]
        gn = ca.take(1024, BF16)
        m2 = ca.take(2048, F32)
        mixa = ca.take(1024, BF16)
        mixm = [ca.take(1024, BF16) for _ in range(2)]
        PT = [ca.take(1024, BF16) for _ in range(4)]
        af = ca.take(2048, F32)
        mT = ca.take(2048, BF16)
        cb = Carve(71 * KB, TOTAL_BYTES)
        W2 = cb.take(64 * KB, BF16).rearrange("p (j d) -> p j d", j=32)
        fT = cb.take(32 * KB, BF16).rearrange("p (j t) -> p j t", j=32)
        xr = [cb.take(4096, F32) for _ in range(6)]
        h2bs = [cb.take(2048, BF16) for _ in range(2)]
        h2T = cb.take(8192, BF16).rearrange("p (c t) -> p c t", c=8)
        rb = [cb.take(1024, BF16) for _ in range(2)]

        def dma1(out, in_, **kw):
            return lambda e: e.dma_start(out=out, in_=in_, **kw)

        def rstd(src, dst, k, scale, eps, rsrc, rdst):
            S.add("pool", lambda e: e.tensor_scalar(out=dst, in0=src, scalar1=scale, scalar2=eps,
                                                    op0=ALU.mult, op1=ALU.add),
                  reads=[rsrc], writes=[rdst])
            S.add("pool", lambda e: e.tensor_tensor(out=dst, in0=dst, in1=neghalf[:, 0:k], op=ALU.pow),
                  reads=[rdst], writes=[rdst])

        def transposes8(src, bank, rsrcs):
            def emit(e):
                last = None
                for c in range(8):
                    last = e.transpose(psb[bank][:, c * 128:(c + 1) * 128], src[:, c * 128:(c + 1) * 128], ident)
                return last
            S.add("pe", emit, reads=list(rsrcs) + ["ident"], writes=[f"ps{bank}"])

        def consts(e):
            return [
                e.dma_start(out=ident, in_=ident_d),
                e.dma_start(out=masks.rearrange("p d q -> p (d q)"), in_=masks_d),
                e.dma_start(out=triu, in_=triu_d),
                e.dma_start(out=g1c, in_=g1c_d),
                e.dma_start(out=goc, in_=goc_d),
                e.dma_start(out=bsT, in_=bsT_d),
                e.dma_start(out=wsT, in_=wsT_d),
                e.dma_start(out=lng, in_=lng_d.partition_broadcast(128)),
                e.dma_start(out=lnb, in_=lnb_d.partition_broadcast(128)),
                e.dma_start(out=gq_t[:, 0:1], in_=gq_d),
                e.dma_start(out=gk_t[:, 0:1], in_=gk_d),
                e.dma_start(out=g2b, in_=g2_d.partition_broadcast(128)),
            ]
        S.add("sp", consts, writes=["ident", "masks", "triu", "g1c", "goc", "bsT", "wsT", "lng", "lnb",
                                    "gq", "gk", "g2b"], dsem=DS_C, n_dma=12)
        S.add("pool", lambda e: e.memset(neghalf, -0.5), writes=["neghalf"])
        S.add("pool", lambda e: e.memset(VA[:, :, :, 64:65], 1.0), writes=["VAones"])
        S.add("dve", lambda e: e.tensor_tensor(out=gqkT[:, 0:1], in0=gq_t[:, 0:1], in1=gk_t[:, 0:1], op=ALU.mult),
              reads=["gq", "gk"], writes=["gqk"])
        S.add("dve", lambda e: e.tensor_tensor(
            out=WmT, in0=wsT.rearrange("p (g t) -> p g t", g=4),
            in1=triu.unsqueeze(1).broadcast_to([128, 4, 128]), op=ALU.mult),
            reads=["wsT", "triu"], writes=["WmT"])

        k = 0
        for c in range(8):
            s = k % 3
            k += 1
            S.add("sp", dma1(stg[s][:, 0:INW], w_in_d[c * 128:(c + 1) * 128, :]),
                  writes=[f"stg{s}"], dsem=DS_STG0 + s)
            if c % 2 == 0:
                S.add("act", lambda e, c=c, s=s: e.activation(out=w_in[:, c, :], in_=stg[s][:, 0:INW],
                                                              func=AF.Copy, scale=g1c[:, c:c + 1]),
                      reads=[f"stg{s}", "g1c"], writes=["w_in"])
            else:
                S.add("dve", lambda e, c=c, s=s: e.tensor_scalar(out=w_in[:, c, :], in0=stg[s][:, 0:INW],
                                                                 scalar1=g1c[:, c:c + 1], scalar2=None,
                                                                 op0=ALU.mult),
                      reads=[f"stg{s}", "g1c"], writes=["w_in"])
        for c in range(8):
            s = k % 3
            k += 1
            S.add("sp", dma1(stg[s][:, 0:D], w_out_d[c * 128:(c + 1) * 128, :]),
                  writes=[f"stg{s}"], dsem=DS_STG0 + s)
            if c % 2 == 0:
                S.add("act", lambda e, c=c, s=s: e.activation(out=w_out[:, c, :], in_=stg[s][:, 0:D],
                                                              func=AF.Copy, scale=goc[:, c:c + 1]),
                      reads=[f"stg{s}", "goc"], writes=["w_out"])
            else:
                S.add("dve", lambda e, c=c, s=s: e.tensor_scalar(out=w_out[:, c, :], in0=stg[s][:, 0:D],
                                                                 scalar1=goc[:, c:c + 1], scalar2=None,
                                                                 op0=ALU.mult),
                      reads=[f"stg{s}", "goc"], writes=["w_out"])

        def w1_load(e):
            return [e.dma_start(out=W1[:, c, hh * 2048:(hh + 1) * 2048],
                                in_=w1_d[c * 128:(c + 1) * 128, hh * 2048:(hh + 1) * 2048])
                    for c in range(8) for hh in range(2)]
        S.add("pool", w1_load, writes=["W1", "stg0", "stg1", "stg2", "wsT", "triu", "gq", "gk"],
              dsem=DS_W1, n_dma=16)

        fbanks = [0, 1, 5]
        fb_i = [0]

        def next_fb():
            b = fbanks[fb_i[0] % 3]
            fb_i[0] += 1
            return b

        def inproj_block(blk, bank):
            def emit(e):
                last = None
                for c in range(8):
                    last = e.matmul(psf[bank], lhsT=hT[:, c * 128:(c + 1) * 128],
                                    rhs=w_in[:, c, blk * 512:(blk + 1) * 512],
                                    start=(c == 0), stop=(c == 7))
                return last
            S.add("pe", emit, reads=["hT", "w_in"], writes=[f"ps{bank}"])

        def gelu_chain(bank, half, ydst_name):
            sl = slice(half * 512, (half + 1) * 512)
            S.add("act", lambda e: e.activation(out=sq[:, sl], in_=psf[bank], func=AF.Square,
                                                scale=float(np.sqrt(GA))),
                  reads=[f"ps{bank}"], writes=[f"sq{half}"])
            S.add("dve", lambda e: e.scalar_tensor_tensor(out=tt, in0=sq[:, sl], scalar=1.0,
                                                          in1=psf[bank], op0=ALU.add, op1=ALU.mult),
                  reads=[f"sq{half}", f"ps{bank}"], writes=["tt"])
            S.add("act", lambda e: e.activation(out=tt, in_=tt, func=AF.Tanh, scale=GC),
                  reads=["tt"], writes=["tt"])
            S.add("dve", lambda e: e.scalar_tensor_tensor(out=y2[:, sl], in0=tt, scalar=1.0,
                                                          in1=psf[bank], op0=ALU.add, op1=ALU.mult),
                  reads=["tt", f"ps{bank}"], writes=[ydst_name])

        def front(t):
            p = t % 2
            b = t % 16
            st = stF[p]
            S.add("sp", dma1(xt[p], x_d[t * 128:(t + 1) * 128, :]), writes=[f"xt{p}"], dsem=DS_XT0 + p)
            yield
            S.add("act", lambda e: e.activation(out=sq, in_=xt[p], func=AF.Square, accum_out=st[:, 0:1]),
                  reads=[f"xt{p}"], writes=["sq0", "sq1", f"ss1_{p}"])
            rstd(st[:, 0:1], st[:, 1:2], 1, 1.0 / D, EPS, f"ss1_{p}", f"r1_{p}")
            S.add("dve", lambda e: e.tensor_scalar(out=hb, in0=xt[p], scalar1=st[:, 1:2], scalar2=None,
                                                   op0=ALU.mult),
                  reads=[f"xt{p}", f"r1_{p}"], writes=["hb0", "hb1"])
            yield
            bk, bq, bkk, bv, bu, bg, bt = 0, 1, 4, 5, 0, 5, 1
            transposes8(hb, bk, ["hb0", "hb1"])
            S.add("act", lambda e: e.copy(out=hT, in_=psb[bk]), reads=[f"ps{bk}"], writes=["hT"])
            yield
            inproj_block(0, bq)
            inproj_block(1, bkk)
            S.add("act", lambda e: e.activation(out=sq[:, 0:512], in_=psf[bq], func=AF.Square),
                  reads=[f"ps{bq}"], writes=["sq0"])
            S.add("act", lambda e: e.activation(out=sq[:, 512:1024], in_=psf[bkk], func=AF.Square),
                  reads=[f"ps{bkk}"], writes=["sq1"])
            S.add("dve", lambda e: e.tensor_reduce(out=st[:, 8:24],
                                                   in_=sq.rearrange("p (h d) -> p h d", d=64),
                                                   axis=AX.X, op=ALU.add),
                  reads=["sq0", "sq1"], writes=[f"ssqk_{p}"])
            rstd(st[:, 8:24], st[:, 24:40], 16, 1.0 / 64, EPS, f"ssqk_{p}", f"rqk_{p}")
            yield
            inproj_block(2, bv)
            S.add("act", lambda e: e.copy(out=VA[:, b, :, 0:64],
                                          in_=psf[bv].rearrange("p (h d) -> p h d", d=64)),
                  reads=[f"ps{bv}", "VAones"], writes=[f"VA{b}"])
            inproj_block(3, bu)
            yield
            inproj_block(4, bg)
            S.add("dve", lambda e: e.tensor_tensor(
                out=qn[:, 0:512].rearrange("p (h d) -> p h d", d=64),
                in0=psf[bq].rearrange("p (h d) -> p h d", d=64),
                in1=st[:, 24:32].unsqueeze(2).broadcast_to([128, 8, 64]), op=ALU.mult),
                reads=[f"ps{bq}", f"rqk_{p}"], writes=["hb0"])
            S.add("dve", lambda e: e.tensor_tensor(
                out=qn[:, 512:1024].rearrange("p (h d) -> p h d", d=64),
                in0=psf[bkk].rearrange("p (h d) -> p h d", d=64),
                in1=st[:, 32:40].unsqueeze(2).broadcast_to([128, 8, 64]), op=ALU.mult),
                reads=[f"ps{bkk}", f"rqk_{p}"], writes=["hb1"])
            yield
            gelu_chain(bu, 0, "y2u")
            yield
            transposes8(qn, bt, ["hb0", "hb1"])
            S.add("act", lambda e: e.copy(out=qT[p].rearrange("p j t -> p (j t)"), in_=psb[bt][:, 0:512]),
                  reads=[f"ps{bt}"], writes=[f"qT{p}"])
            S.add("dve", lambda e: e.tensor_scalar(
                out=KT[:, :, b * 128:(b + 1) * 128],
                in0=psb[bt][:, 512:1024].rearrange("p (j t) -> p j t", j=4),
                scalar1=gqkT[:, 0:1], scalar2=None, op0=ALU.mult),
                reads=[f"ps{bt}", "gqk"], writes=[f"KT{b}"])
            yield
            gelu_chain(bg, 1, "y2g")
            yield
            y2g = y2[:, 512:1024].rearrange("p (g c) -> p g c", g=4)
            bst = st[:, 40:64].rearrange("p (g s) -> p g s", g=4)

            def bn_s(e):
                last = None
                for g_ in range(4):
                    last = e.bn_stats(out=bst[:, g_, :], in_=y2g[:, g_, :])
                return last
            S.add("dve", bn_s, reads=["y2g"], writes=[f"bst_{p}"])
            mv = st[:, 64:72].rearrange("p (g s) -> p g s", g=4)

            def bn_a(e):
                last = None
                for g_ in range(4):
                    last = e.bn_aggr(out=mv[:, g_, :], in_=bst[:, g_, :])
                return last
            S.add("dve", bn_a, reads=[f"bst_{p}"], writes=[f"mv_{p}"])
            rg = st[:, 72:76]
            rstd(mv[:, :, 1], rg, 4, 1.0, 4 * EPS, f"mv_{p}", f"rg_{p}")
            gnf = sq[:, 0:512].rearrange("p (g c) -> p g c", g=4)

            def ln_apply(e):
                last = None
                for g_ in range(4):
                    last = e.tensor_scalar(out=gnf[:, g_, :], in0=y2g[:, g_, :], scalar1=mv[:, g_, 0:1],
                                           scalar2=rg[:, g_:g_ + 1], op0=ALU.subtract, op1=ALU.mult)
                return last
            S.add("dve", ln_apply, reads=["y2g", f"mv_{p}", f"rg_{p}"], writes=["sq0"])
            S.add("pool", lambda e: e.tensor_tensor(out=sq[:, 0:512], in0=sq[:, 0:512], in1=lng, op=ALU.mult),
                  reads=["sq0", "lng"], writes=["sq0"])
            S.add("dve", lambda e: e.tensor_tensor(out=gn, in0=sq[:, 0:512], in1=lnb, op=ALU.add),
                  reads=["sq0", "lnb"], writes=["gn"])
            yield
            bz = 4

            def spat(e):
                last = None
                for g_ in range(4):
                    last = e.matmul(psf[bz][:, g_ * 128:(g_ + 1) * 128], lhsT=WmT[:, g_, :],
                                    rhs=gn[:, g_ * 128:(g_ + 1) * 128], start=True, stop=True,
                                    skip_group_check=True)
                return last
            S.add("pe", spat, reads=["gn", "WmT"], writes=[f"ps{bz}"])

            def gate(e):
                last = None
                for g_ in range(4):
                    sl = slice(g_ * 128, (g_ + 1) * 128)
                    last = e.scalar_tensor_tensor(out=m2[:, sl], in0=psf[bz][:, sl], scalar=bsT[:, g_:g_ + 1],
                                                  in1=y2[:, sl], op0=ALU.add, op1=ALU.mult)
                return last
            S.add("dve", gate, reads=[f"ps{bz}", "bsT", "y2u"], writes=["m2"])
            S.add("act", lambda e: e.activation(out=sq[:, 512:1024], in_=m2, func=AF.Square,
                                                accum_out=st[:, 2:3]),
                  reads=["m2"], writes=["sq1", f"ssm_{p}"])
            rstd(st[:, 2:3], st[:, 3:4], 1, 1.0 / 512, 4 * EPS, f"ssm_{p}", f"rm_{p}")
            S.add("dve", lambda e: e.tensor_scalar(out=mixm[p], in0=m2, scalar1=st[:, 3:4],
                                                   scalar2=None, op0=ALU.mult),
                  reads=["m2", f"rm_{p}"], writes=[f"mixm{p}"])

        out_ops = []
        sbanks = [2, 3]
        sb_i = [0]
        pt_i = [0]

        def back(t):
            p = t % 2
            b = t % 16
            st = stB[p]
            pend = []

            def unit_qk(kb, par):
                bank = sbanks[sb_i[0] % 2]
                sb_i[0] += 1
                slot = pt_i[0] % 4
                pt_i[0] += 1

                def emit(e):
                    last = None
                    for j in range(4):
                        rows = slice(par * 64, par * 64 + 64)
                        last = e.matmul(psf[bank][:, j * 128:(j + 1) * 128],
                                        lhsT=KT[rows, j, kb * 128:(kb + 1) * 128],
                                        rhs=qT[p][rows, j, :], start=True, stop=True, skip_group_check=True)
                    return last
                S.add("pe", emit, reads=[f"KT{kb}", f"qT{p}"], writes=[f"ps{bank}"])
                S.add("act", lambda e: e.activation(out=PT[slot], in_=psf[bank], func=AF.Exp, scale=0.125),
                      reads=[f"ps{bank}"], writes=[f"PT{slot}"])
                dd = b - kb
                S.add("dve", lambda e: e.tensor_tensor(
                    out=PT[slot].rearrange("p (j q) -> p j q", j=4),
                    in0=PT[slot].rearrange("p (j q) -> p j q", j=4),
                    in1=masks[:, dd, :].unsqueeze(1).broadcast_to([128, 4, 128]), op=ALU.mult),
                    reads=[f"PT{slot}", "masks"], writes=[f"PT{slot}"])
                return slot

            def unit_pv(kb, par, slot):
                def emit(e):
                    last = None
                    for j in range(4):
                        h = 2 * j + par
                        bank = 6 if h < 4 else 7
                        col = (h % 4) * 65
                        first = (kb == 0 and par == 0 and (h == 0 or h == 4))
                        last = e.matmul(psf[bank][:, col:col + 65], lhsT=PT[slot][:, j * 128:(j + 1) * 128],
                                        rhs=VA[:, kb, h, :], start=first, stop=(kb == b),
                                        skip_group_check=True)
                    return last
                S.add("pe", emit, reads=[f"PT{slot}", f"VA{kb}"], writes=["ps6", "ps7"])

            for kb in range(b + 1):
                cur = []
                for par in range(2):
                    cur.append((kb, par, unit_qk(kb, par)))
                for (k_, p_, s_) in pend:
                    unit_pv(k_, p_, s_)
                pend = cur
                yield
            for (k_, p_, s_) in pend:
                unit_pv(k_, p_, s_)
            rden = st[:, 0:8]

            def rd(e):
                i0 = e.reciprocal(out=rden[:, 0:4].unsqueeze(2),
                                  in_=psf[6][:, 0:260].rearrange("p (h e) -> p h e", e=65)[:, :, 64:65])
                i1 = e.reciprocal(out=rden[:, 4:8].unsqueeze(2),
                                  in_=psf[7][:, 0:260].rearrange("p (h e) -> p h e", e=65)[:, :, 64:65])
                return i1
            S.add("dve", rd, reads=["ps6", "ps7"], writes=[f"rden_{p}"])

            def an(e):
                last = None
                for hb_, bank in ((0, 6), (1, 7)):
                    last = e.tensor_tensor(
                        out=af[:, hb_ * 256:(hb_ + 1) * 256].rearrange("p (h d) -> p h d", d=64),
                        in0=psf[bank][:, 0:260].rearrange("p (h e) -> p h e", e=65)[:, :, 0:64],
                        in1=rden[:, hb_ * 4:(hb_ + 1) * 4].unsqueeze(2).broadcast_to([128, 4, 64]),
                        op=ALU.mult)
                return last
            S.add("dve", an, reads=["ps6", "ps7", f"rden_{p}"], writes=["af"])
            S.add("act", lambda e: e.activation(out=mT[:, 0:512], in_=af, func=AF.Square, accum_out=st[:, 8:9]),
                  reads=["af"], writes=["mTjunk", f"ssa_{p}"])
            rstd(st[:, 8:9], st[:, 9:10], 1, 1.0 / 512, EPS, f"ssa_{p}", f"ra_{p}")
            S.add("dve", lambda e: e.tensor_scalar(out=mixa, in0=af, scalar1=st[:, 9:10],
                                                   scalar2=None, op0=ALU.mult),
                  reads=["af", f"ra_{p}"], writes=["mixa"])
            yield
            bank = sbanks[sb_i[0] % 2]
            sb_i[0] += 1

            def tr(e):
                last = None
                for c in range(8):
                    src = mixa[:, c * 128:(c + 1) * 128] if c < 4 else mixm[p][:, (c - 4) * 128:(c - 3) * 128]
                    last = e.transpose(psb[bank][:, c * 128:(c + 1) * 128], src, ident)
                return last
            S.add("pe", tr, reads=["mixa", f"mixm{p}", "ident"], writes=[f"ps{bank}"])
            S.add("act", lambda e: e.copy(out=mT, in_=psb[bank]), reads=[f"ps{bank}"], writes=["mT", "mTjunk"])
            yield
            for blk in range(2):
                bank2 = sbanks[sb_i[0] % 2]
                sb_i[0] += 1

                def op_(e, blk=blk, bank2=bank2):
                    last = None
                    for c in range(8):
                        last = e.matmul(psf[bank2], lhsT=mT[:, c * 128:(c + 1) * 128],
                                        rhs=w_out[:, c, blk * 512:(blk + 1) * 512],
                                        start=(c == 0), stop=(c == 7))
                    return last
                S.add("pe", op_, reads=["mT", "w_out"], writes=[f"ps{bank2}"])
                S.add("dve", lambda e, blk=blk, bank2=bank2: e.tensor_tensor(
                    out=xt[p][:, blk * 512:(blk + 1) * 512], in0=psf[bank2],
                    in1=xt[p][:, blk * 512:(blk + 1) * 512], op=ALU.add),
                    reads=[f"ps{bank2}", f"xt{p}"], writes=[f"xt{p}"])
            dst_ = x1_d if do_b else y_d
            o_ = S.add("sp", dma1(dst_[t * 128:(t + 1) * 128, :], xt[p]), reads=[f"xt{p}"], writes=[f"x1d{t}"],
                       dsem=DS_XT0 + p)
            if not do_b:
                out_ops.append(o_)

        def mark(name):
            if marks is not None:
                marks.append((name, len(S.ops)))

        def run(gen):
            for _ in gen:
                mark("step")

        def interleave(g1, g2):
            gens = [g for g in (g1, g2) if g is not None]
            while gens:
                for g in list(gens):
                    try:
                        next(g)
                    except StopIteration:
                        gens.remove(g)

        mark("prologue_end")
        run(front(0))
        mark("front0_end")
        for t in range(NT):
            nxt = t + 1
            if nxt < NT and nxt % 16 != 0:
                interleave(back(t), front(nxt))
            else:
                run(back(t))
                if nxt < NT:
                    run(front(nxt))
            mark(f"tile{t}_end")

        mark("phaseA_end")
        S.barrier()
        for q4 in range(4 if do_b else 0):
            S.add("pool", lambda e, q4=q4: [e.dma_start(
                out=W2[:, q4 * 8 + jj, :], in_=w2_d[(q4 * 8 + jj) * 128:(q4 * 8 + jj + 1) * 128, :])
                for jj in range(8)], writes=[f"W2_{q4}"], dsem=DS_W2_0 + q4, n_dma=8)

        ring_i = [0]
        f1banks = [0, 1, 2, 3]
        f2banks = [4, 5]
        trbanks = [6, 7]
        cnt1 = [0]
        cnt2 = [0]
        cntt = [0]
        NG = NT // 4 if do_b else 0
        slot_of = {}
        next_load = [0]

        def load_next():
            t = next_load[0]
            if t >= NG * 4:
                return
            next_load[0] += 1
            r = ring_i[0] % 6
            ring_i[0] += 1
            slot_of[t] = r
            S.add("sp", dma1(xr[r], x1_d[t * 128:(t + 1) * 128, :]), reads=[f"x1d{t}"], writes=[f"xr{r}"],
                  dsem=DS_R0 + r)

        def prep_tile(t):
            i = t % 4
            r = slot_of[t]
            hs = t % 2
            hbuf = h2bs[hs]
            ss = stP[:, 2 * i:2 * i + 1]
            rr = stP[:, 2 * i + 1:2 * i + 2]
            S.add("act", lambda e: e.activation(out=hbuf, in_=xr[r], func=AF.Square, accum_out=ss),
                  reads=[f"xr{r}"], writes=[f"h2b{hs}", f"ss2_{i}"])
            rstd(ss, rr, 1, 1.0 / D, EPS, f"ss2_{i}", f"r2_{i}")
            S.add("dve", lambda e: e.scalar_tensor_tensor(out=hbuf, in0=xr[r], scalar=rr, in1=g2b,
                                                          op0=ALU.mult, op1=ALU.mult),
                  reads=[f"xr{r}", f"r2_{i}", "g2b"], writes=[f"h2b{hs}"])
            bank = trbanks[cntt[0] % 2]
            cntt[0] += 1
            transposes8(hbuf, bank, [f"h2b{hs}"])
            S.add("act", lambda e: e.copy(out=h2T[:, :, i * 128:(i + 1) * 128],
                                          in_=psb[bank].rearrange("p (c t) -> p c t", c=8)),
                  reads=[f"ps{bank}"], writes=["h2T"])

        def ff1(gi):
            for j in range(32):
                bank = f1banks[cnt1[0] % 4]
                cnt1[0] += 1
                rslot = j % 2

                def f1(e, j=j, bank=bank):
                    last = None
                    for c in range(8):
                        last = e.matmul(psf[bank], lhsT=W1[:, c, j * 128:(j + 1) * 128], rhs=h2T[:, c, :],
                                        start=(c == 0), stop=(c == 7))
                    return last
                S.add("pe", f1, reads=["W1", "h2T"], writes=[f"ps{bank}"])
                S.add("act", lambda e, bank=bank, rslot=rslot: e.activation(out=rb[rslot], in_=psf[bank],
                                                                            func=AF.Relu),
                      reads=[f"ps{bank}"], writes=[f"rb{rslot}"])
                S.add("dve", lambda e, j=j, bank=bank, rslot=rslot: e.scalar_tensor_tensor(
                    out=fT[:, j, :], in0=psf[bank], scalar=0.0, in1=rb[rslot], op0=ALU.max, op1=ALU.mult),
                    reads=[f"ps{bank}", f"rb{rslot}"], writes=[f"fT{j}"])

        def ff2_tile(gi, i):
            t = gi * 4 + i
            r = slot_of[t]
            for blk in range(2):
                bank = f2banks[cnt2[0] % 2]
                cnt2[0] += 1

                def f2(e, blk=blk, bank=bank):
                    last = None
                    for j in range(32):
                        last = e.matmul(psf[bank], lhsT=fT[:, j, i * 128:(i + 1) * 128],
                                        rhs=W2[:, j, blk * 512:(blk + 1) * 512],
                                        start=(j == 0), stop=(j == 31))
                    return last
                S.add("pe", f2, reads=[f"fT{j}" for j in range(32)] + [f"W2_{q}" for q in range(4)],
                      writes=[f"ps{bank}"])
                S.add("dve", lambda e, blk=blk, bank=bank: e.tensor_tensor(
                    out=xr[r][:, blk * 512:(blk + 1) * 512], in0=psf[bank],
                    in1=xr[r][:, blk * 512:(blk + 1) * 512], op=ALU.add),
                    reads=[f"ps{bank}", f"xr{r}"], writes=[f"xr{r}"])
            out_ops.append(S.add("sp", dma1(y_d[t * 128:(t + 1) * 128, :], xr[r]), reads=[f"xr{r}"],
                                 writes=[f"y{t}"], dsem=DS_R0 + r))

        if NG:
            for _ in range(6):
                load_next()
            for t in range(4):
                prep_tile(t)
        for gi in range(NG):
            ff1(gi)
            for i in range(4):
                ff2_tile(gi, i)
                load_next()
                if gi + 1 < NG:
                    prep_tile((gi + 1) * 4 + i)
        S.add("sp", None, extra_deps=out_ops)

        if max_ops is not None:
            S.ops = S.ops[:max_ops]
            lasts = {}
            for o in S.ops:
                lasts[S._key(o)] = o
            S.add("sp", None, extra_deps=[o for o in lasts.values() if o.emit is not None])
        S.finalize()

        @block.sync
        def _(e):
            S.emit_engine("sp", e, esems, dsems)

        @block.tensor
        def _(e):
            S.emit_engine("pe", e, esems, dsems)

        @block.scalar
        def _(e):
            S.emit_engine("act", e, esems, dsems)

        @block.vector
        def _(e):
            S.emit_engine("dve", e, esems, dsems)

        @block.gpsimd
        def _(e):
            S.emit_engine("pool", e, esems, dsems)

    return nc


_CONST_CACHE = {}


def _consts():
    if "c" not in _CONST_CACHE:
        kl = np.arange(128)[:, None]
        ql = np.arange(128)[None, :]
        masks = np.zeros((128, 16, 128), np.float32)
        for dd in range(16):
            dist = 128 * dd + ql - kl
            ge = dist >= 0
            m = (ge & (dist <= 128)).astype(np.float32)
            m += (ge & (dist % 4 == 0) & (dist <= 512)).astype(np.float32)
            m += (ge & (dist % 16 == 0) & (dist <= 2048)).astype(np.float32)
            masks[:, dd, :] = m
        masks = masks.reshape(128, 16 * 128).astype(ml_dtypes.bfloat16)
        ident = np.eye(128, dtype=np.float32).astype(ml_dtypes.bfloat16)
        triu = (kl <= ql).astype(np.float32)
        _CONST_CACHE["c"] = (masks, ident, triu)
    return _CONST_CACHE["c"]


def _prep(x, norm1_g, w_in, q_norm_g, k_norm_g, ln_v_g, ln_v_b, w_spatial, b_spatial, attn_out_g,
          gmlp_out_g, w_out, norm2_g, w_ff1, w_ff2):
    f = lambda a: np.ascontiguousarray(np.asarray(a, dtype=np.float32))
    x2 = f(x).reshape(-1, D)
    masks, ident, triu = _consts()
    shared = {
        "w_in": f(w_in)[0], "w_out": f(w_out)[0], "w_ff1": f(w_ff1)[0], "w_ff2": f(w_ff2)[0],
        "g1c": np.ascontiguousarray(f(norm1_g)[0].reshape(8, 128).T),
        "goc": np.ascontiguousarray(np.concatenate([f(attn_out_g)[0], f(gmlp_out_g)[0]]).reshape(8, 128).T),
        "bsT": np.ascontiguousarray(f(b_spatial)[0].T),
        "wsT": np.ascontiguousarray(f(w_spatial)[0].transpose(2, 0, 1).reshape(128, 512)),
        "lng": f(ln_v_g)[0].reshape(1, 512), "lnb": f(ln_v_b)[0].reshape(1, 512),
        "gqc": np.ascontiguousarray(np.tile(f(q_norm_g)[0], 2).reshape(128, 1)),
        "gkc": np.ascontiguousarray(np.tile(f(k_norm_g)[0], 2).reshape(128, 1)),
        "g2": f(norm2_g)[0].reshape(1, D),
        "masks": masks, "ident": ident, "triu": triu,
    }
    return x2, shared


def kernel(x, norm1_g, w_in, q_norm_g, k_norm_g, ln_v_g, ln_v_b, w_spatial, b_spatial, attn_out_g,
           gmlp_out_g, w_out, norm2_g, w_ff1, w_ff2):
    x2, shared = _prep(x, norm1_g, w_in, q_norm_g, k_norm_g, ln_v_g, ln_v_b, w_spatial, b_spatial,
                       attn_out_g, gmlp_out_g, w_out, norm2_g, w_ff1, w_ff2)
    nc = build_program()
    in_maps = []
    for c in range(NCORES):
        m = dict(shared)
        m["x"] = x2[c * TOK:(c + 1) * TOK]
        in_maps.append(m)
    res = run_bass_kernel_spmd(nc, in_maps, core_ids=list(range(NCORES)))
    out = np.concatenate([np.asarray(r["y"], dtype=np.float32) for r in res.results], axis=0)
    return out.reshape(16, 2048, D)
```
